# Optimizing a Trainium2 kernel written in Bass

```python
import math
import jax, jax.numpy as jnp
from jax import lax
import numpy as np

D_MODEL = 2048
BATCH = 1
SEQ = 8192
DEPTH = 2

HEAD_DIM = 64
MIX_WIDTH = D_MODEL
SWA_WIDTH = 3 * MIX_WIDTH // 8
SWA_HEADS = SWA_WIDTH // HEAD_DIM
SWA_KV_HEADS = SWA_HEADS // 3
SWA_GROUP = SWA_HEADS // SWA_KV_HEADS
WINDOW = 128
SWA_BLOCK = 128
REL_BUCKETS = 32
REL_MAX_DIST = 128
GLA_WIDTH = 3 * MIX_WIDTH // 8
GLA_HEADS = 4
GLA_DV = GLA_WIDTH // GLA_HEADS
GLA_DK = GLA_DV // 2
GLA_LOWRANK = 16
GLA_CHUNK = 64
GATE_NORMALIZER = 16.0
GATE_LOG_MIN = -1.0
SB_WIDTH = MIX_WIDTH - SWA_WIDTH - GLA_WIDTH
SB_HEADS = SB_WIDTH // HEAD_DIM
SB_BLOCK = 128
D_FF = 4 * D_MODEL
RMS_EPS = 1e-6
NEG_INF = -1e30

PROJ_SIZES = (
    SWA_WIDTH, SWA_KV_HEADS * HEAD_DIM, SWA_KV_HEADS * HEAD_DIM,
    GLA_HEADS * GLA_DK, GLA_HEADS * GLA_DK, GLA_WIDTH, GLA_WIDTH, GLA_LOWRANK,
    SB_WIDTH, SB_WIDTH, SB_WIDTH,
)
PROJ_WIDTH = sum(PROJ_SIZES)

kernel_name = 'hybrid_swa_gla_stickbreak_block'


def rmsnorm(x, gain, eps=RMS_EPS):
    xf = x.astype(jnp.float32)
    y = xf * lax.rsqrt(jnp.mean(xf * xf, axis=-1, keepdims=True) + eps)
    return (y * gain.astype(jnp.float32)).astype(x.dtype)


def t5_causal_bucket(dist):
    max_exact = REL_BUCKETS // 2
    is_small = dist < max_exact
    ratio = jnp.log(jnp.maximum(dist, 1).astype(jnp.float32) / max_exact) / math.log(REL_MAX_DIST / max_exact)
    large = max_exact + (ratio * (REL_BUCKETS - max_exact)).astype(jnp.int32)
    large = jnp.minimum(large, REL_BUCKETS - 1)
    return jnp.where(is_small, dist, large)


def swa_sink_attention(q, k, v, sinks, rel_bias):
    B, S = q.shape[:2]
    nb = S // SWA_BLOCK
    qb = q.reshape(B, nb, SWA_BLOCK, SWA_KV_HEADS, SWA_GROUP, HEAD_DIM)
    kb = k.reshape(B, nb, SWA_BLOCK, SWA_KV_HEADS, HEAD_DIM)
    vb = v.reshape(B, nb, SWA_BLOCK, SWA_KV_HEADS, HEAD_DIM)
    pad = ((0, 0), (1, 0), (0, 0), (0, 0), (0, 0))
    kw = jnp.concatenate([jnp.pad(kb, pad)[:, :-1], kb], axis=2)
    vw = jnp.concatenate([jnp.pad(vb, pad)[:, :-1], vb], axis=2)
    logits = jnp.einsum('bnqhgd,bnkhd->bnhgqk', qb, kw).astype(jnp.float32) * (HEAD_DIM ** -0.5)
    qpos = jnp.arange(SWA_BLOCK) + SWA_BLOCK
    kpos = jnp.arange(2 * SWA_BLOCK)
    dist = qpos[:, None] - kpos[None, :]
    in_window = (dist >= 0) & (dist < WINDOW)
    key_abs = jnp.arange(nb)[:, None] * SWA_BLOCK - SWA_BLOCK + kpos[None, :]
    mask = in_window[None, :, :] & (key_abs >= 0)[:, None, :]
    bias = rel_bias.astype(jnp.float32)[t5_causal_bucket(jnp.maximum(dist, 0))]
    bias = bias.transpose(2, 0, 1).reshape(SWA_KV_HEADS, SWA_GROUP, SWA_BLOCK, 2 * SWA_BLOCK)
    logits = jnp.where(mask[None, :, None, None], logits + bias[None, None], NEG_INF)
    sink_col = jnp.broadcast_to(
        sinks.astype(jnp.float32).reshape(1, 1, SWA_KV_HEADS, SWA_GROUP, 1, 1),
        logits.shape[:-1] + (1,))
    probs = jax.nn.softmax(jnp.concatenate([logits, sink_col], axis=-1), axis=-1)[..., :-1]
    out = jnp.einsum('bnhgqk,bnkhd->bnqhgd', probs.astype(v.dtype), vw)
    return out.reshape(B, S, SWA_HEADS * HEAD_DIM)


def gla_chunked(q, k, v, log_gate):
    B, S, H, dk = q.shape
    dv = v.shape[-1]
    n = S // GLA_CHUNK
    f32 = jnp.float32
    q = (q.astype(f32) * dk ** -0.5).reshape(B, n, GLA_CHUNK, H, dk)
    k = k.astype(f32).reshape(B, n, GLA_CHUNK, H, dk)
    v = v.astype(f32).reshape(B, n, GLA_CHUNK, H, dv)
    b = jnp.cumsum(log_gate.reshape(B, n, GLA_CHUNK, H, dk), axis=2)
    b_last = b[:, :, -1:]
    q_dec = q * jnp.exp(b)
    k_inv = k * jnp.exp(-b)
    k_end = k * jnp.exp(b_last - b)
    causal = jnp.tril(jnp.ones((GLA_CHUNK, GLA_CHUNK), dtype=bool))
    scores = jnp.where(causal, jnp.einsum('bnthk,bnshk->bnhts', q_dec, k_inv), 0.0)
    o_intra = jnp.einsum('bnhts,bnshv->bnthv', scores, v)
    d_state = jnp.einsum('bnshk,bnshv->bnhkv', k_end, v)
    decay = jnp.exp(b_last[:, :, 0])

    def step(state, inp):
        dec, ds = inp
        return dec[..., None] * state + ds, state

    init = jnp.zeros((B, H, dk, dv), f32)
    _, s_prev = lax.scan(step, init, (decay.swapaxes(0, 1), d_state.swapaxes(0, 1)))
    s_prev = s_prev.swapaxes(0, 1)
    o_inter = jnp.einsum('bnthk,bnhkv->bnthv', q_dec, s_prev)
    return (o_intra + o_inter).reshape(B, S, H, dv)


def stick_breaking_attention(q, k, v):
    B, S, H, D = q.shape
    nb = S // SB_BLOCK
    qb = q.reshape(B, nb, SB_BLOCK, H, D).swapaxes(0, 1)
    kpos = jnp.arange(S)
    scale = D ** -0.5

    def block(args):
        q_blk, i = args
        z = jnp.einsum('bqhd,bshd->bhqs', q_blk, k).astype(jnp.float32) * scale
        qpos = i * SB_BLOCK + jnp.arange(SB_BLOCK)
        strict = kpos[None, :] < qpos[:, None]
        log_beta = jax.nn.log_sigmoid(z)
        log_1m_beta = jnp.where(strict, jax.nn.log_sigmoid(-z), 0.0)
        between = lax.cumsum(log_1m_beta, axis=3, reverse=True) - log_1m_beta
        w = jnp.where(strict, jnp.exp(log_beta + between), 0.0)
        return jnp.einsum('bhqs,bshd->bqhd', w.astype(v.dtype), v)

    out = lax.map(block, (qb, jnp.arange(nb)))
    return out.swapaxes(0, 1).reshape(B, S, H * D)


def token_mixing(h, w_in, sinks, rel_bias, gla_gate_w, gla_gate_b, gla_norm,
                 swa_out_norm, sb_out_norm, w_out):
    B, S, _ = h.shape
    proj = h @ w_in
    split_at = np.cumsum(PROJ_SIZES)[:-1].tolist()
    aq, ak, av, bq, bk, bv, br, blr, cq, ck, cv = jnp.split(proj, split_at, axis=-1)
    out_a = swa_sink_attention(
        aq.reshape(B, S, SWA_HEADS, HEAD_DIM),
        ak.reshape(B, S, SWA_KV_HEADS, HEAD_DIM),
        av.reshape(B, S, SWA_KV_HEADS, HEAD_DIM), sinks, rel_bias)
    out_a = rmsnorm(out_a, swa_out_norm)
    gate_pre = (blr @ gla_gate_w + gla_gate_b).astype(jnp.float32)
    log_gate = jnp.maximum(jax.nn.log_sigmoid(gate_pre) / GATE_NORMALIZER, GATE_LOG_MIN)
    o_b = gla_chunked(
        bq.reshape(B, S, GLA_HEADS, GLA_DK),
        bk.reshape(B, S, GLA_HEADS, GLA_DK),
        bv.reshape(B, S, GLA_HEADS, GLA_DV),
        log_gate.reshape(B, S, GLA_HEADS, GLA_DK))
    o_b = rmsnorm(o_b, gla_norm).reshape(B, S, GLA_WIDTH).astype(h.dtype)
    out_b = o_b * jax.nn.silu(br)
    out_c = stick_breaking_attention(
        cq.reshape(B, S, SB_HEADS, HEAD_DIM),
        ck.reshape(B, S, SB_HEADS, HEAD_DIM),
        cv.reshape(B, S, SB_HEADS, HEAD_DIM))
    out_c = rmsnorm(out_c, sb_out_norm)
    mixed = jnp.concatenate([out_a, out_b, out_c], axis=-1)
    return mixed @ w_out


def squared_relu_mlp(h, w1, w2):
    return jnp.square(jax.nn.relu(h @ w1)) @ w2


def setup_inputs(seed: int = 0) -> dict:
    key = jax.random.key(seed)
    ks = jax.random.split(key, 15)
    f32 = jnp.float32

    def nrm(k, shape, scale):
        return jax.random.normal(k, shape, f32) * scale

    def gain(k, shape):
        return 1.0 + 0.05 * jax.random.normal(k, shape, f32)

    return {
        'x': nrm(ks[0], (BATCH, SEQ, D_MODEL), 1.0),
        'norm_mix': gain(ks[1], (DEPTH, D_MODEL)),
        'w_in': nrm(ks[2], (DEPTH, D_MODEL, PROJ_WIDTH), D_MODEL ** -0.5),
        'swa_sinks': nrm(ks[3], (DEPTH, SWA_HEADS), 0.5),
        'rel_bias': nrm(ks[4], (REL_BUCKETS, SWA_HEADS), 0.5),
        'gla_gate_w': nrm(ks[5], (DEPTH, GLA_LOWRANK, GLA_HEADS * GLA_DK), GLA_LOWRANK ** -0.5),
        'gla_gate_b': nrm(ks[6], (DEPTH, GLA_HEADS * GLA_DK), 0.1),
        'gla_norm': gain(ks[7], (DEPTH, GLA_DV)),
        'swa_out_norm': gain(ks[8], (DEPTH, SWA_WIDTH)),
        'sb_out_norm': gain(ks[9], (DEPTH, SB_WIDTH)),
        'w_out': nrm(ks[10], (DEPTH, MIX_WIDTH, D_MODEL), MIX_WIDTH ** -0.5),
        'norm_mlp': gain(ks[11], (DEPTH, D_MODEL)),
        'w_mlp_in': nrm(ks[12], (DEPTH, D_MODEL, D_FF), D_MODEL ** -0.5),
        'w_mlp_out': nrm(ks[13], (DEPTH, D_FF, D_MODEL), D_FF ** -0.5),
        'norm_final': gain(ks[14], (D_MODEL,)),
    }


def reference(x, norm_mix, w_in, swa_sinks, rel_bias, gla_gate_w, gla_gate_b, gla_norm,
              swa_out_norm, sb_out_norm, w_out, norm_mlp, w_mlp_in, w_mlp_out, norm_final):
    for l in range(DEPTH):
        h = rmsnorm(x, norm_mix[l])
        x = x + token_mixing(h, w_in[l], swa_sinks[l], rel_bias, gla_gate_w[l], gla_gate_b[l],
                             gla_norm[l], swa_out_norm[l], sb_out_norm[l], w_out[l])
        h = rmsnorm(x, norm_mlp[l])
        x = x + squared_relu_mlp(h, w_mlp_in[l], w_mlp_out[l])
    return rmsnorm(x, norm_final)
```

```python
import contextlib
import numpy as np
import ml_dtypes
import concourse.bass as bass
import concourse.mybir as mybir
from concourse.bass_utils import run_bass_kernel_spmd

F32 = mybir.dt.float32
BF16 = mybir.dt.bfloat16
ALU = mybir.AluOpType
AF = mybir.ActivationFunctionType
NPBF = ml_dtypes.bfloat16

ENGS = ("pe", "act", "dve", "pool", "sp")
EPOCH = 20000
CC_INC = 1

NCORES = 8
SEQ = 8192
T = SEQ // NCORES
D = 2048
KC = D // 128
DFF = 8192
PROJ_W = 5136
EPS = 1e-6


class Sched:
    def __init__(self, nc, same_engine_sync=True):
        self.nc = nc
        self.stack = contextlib.ExitStack()
        self.streams = {e: [] for e in ENGS}
        self.nops = {e: 0 for e in ENGS}
        self.esems = {e: [] for e in ENGS}
        self.dsems = {}
        self.bufs = {}
        self.waited = {e: {} for e in ENGS}
        self.same_engine_sync = same_engine_sync
        self.out_toks = []
        self.last_tok = {}

    def sem(self, name):
        return self.stack.enter_context(self.nc.semaphore(name))

    def sbuf(self, name, shape, dt):
        return self.stack.enter_context(self.nc.sbuf_tensor(name, list(shape), dt))

    def psum(self, name, shape, dt=F32):
        return self.stack.enter_context(self.nc.psum_tensor(name, list(shape), dt))

    def _eng_token(self, e):
        self.nops[e] += 1
        n = self.nops[e]
        ep, v = divmod(n - 1, EPOCH)
        while len(self.esems[e]) <= ep:
            self.esems[e].append(self.sem(f"s_{e}_{len(self.esems[e])}"))
        tok = (self.esems[e][ep], v + 1, e, f"{e}{ep}")
        self.last_tok[f"{e}{ep}"] = tok
        return tok

    def _filter(self, e, deps):
        out = []
        for tok in deps:
            sem, val, prod, sname = tok
            if prod == e and (e in ("pe", "sp") or not self.same_engine_sync):
                continue
            if self.waited[e].get(sname, 0) >= val:
                continue
            self.waited[e][sname] = val
            out.append((sem, val))
        return out

    def _deps(self, e, reads, writes):
        deps = []
        for k in reads:
            b = self.bufs.get(k)
            if b and b[0] is not None:
                deps.append(b[0])
        for k in writes:
            b = self.bufs.get(k)
            if b:
                if b[0] is not None:
                    deps.append(b[0])
                deps.extend(b[1])
        return self._filter(e, deps)

    def _commit(self, tok, reads, writes):
        for k in reads:
            self.bufs.setdefault(k, [None, []])[1].append(tok)
        for k in writes:
            self.bufs[k] = [tok, []]

    def op(self, e, meth, kw, reads=(), writes=()):
        waits = self._deps(e, reads, writes)
        tok = self._eng_token(e)
        self.streams[e].append((waits, [(meth, kw)], (tok[0], 1)))
        self._commit(tok, reads, writes)
        return tok

    def dma(self, e, kws, reads=(), writes=(), semkey=None, is_out=False):
        if isinstance(kws, dict):
            kws = [kws]
        n = len(kws)
        fn = [("indirect_dma_start" if "in_offset" in kw else "dma_start", kw) for kw in kws]
        waits = self._deps(e, reads, writes)
        if semkey is None:
            semkey = writes[0] if writes else reads[0]
        if semkey not in self.dsems:
            self.dsems[semkey] = [self.sem(f"d{len(self.dsems)}"), 0]
        ds = self.dsems[semkey]
        ds[1] += 16 * n
        sname = f"dma_{semkey}"
        tok = (ds[0], ds[1], "dma", sname)
        self.last_tok[sname] = tok
        self.streams[e].append((waits, fn, (ds[0], 16)))
        self._commit(tok, reads, writes)
        if is_out:
            self.out_toks.append(tok)
        return tok

    def cc(self, kind, ins, outs, reads=(), writes=(), semkey=None):
        waits = self._deps("pool", reads, writes)
        if semkey not in self.dsems:
            self.dsems[semkey] = [self.sem(f"c{len(self.dsems)}"), 0]
        ds = self.dsems[semkey]
        ds[1] += CC_INC
        sname = f"dma_{semkey}"
        tok = (ds[0], ds[1], "dma", sname)
        self.last_tok[sname] = tok
        kw = dict(kind=kind, op=ALU.bypass, replica_groups=[list(range(NCORES))], ins=ins, outs=outs)
        self.streams["pool"].append((waits, [("collective_compute", kw)], (ds[0], CC_INC)))
        self._commit(tok, reads, writes)
        return tok

    def barrier(self):
        toks = list(self.last_tok.values())
        for e in ENGS:
            w = self._filter(e, [t for t in toks if not (t[2] == e and e in ("pe", "sp"))])
            if w:
                self.streams[e].append((w, None, None))

    def emit(self):
        nc = self.nc
        seen = {}
        for sem, val, _, sname in self.out_toks:
            seen[sname] = (sem, max(val, seen.get(sname, (None, 0))[1]))
        fin = list(seen.values())
        streams = self.streams

        def run(eng, items, final=None):
            for waits, fn, si in items:
                for (ws, wv) in waits:
                    eng.wait_ge(ws, wv)
                if fn is None:
                    continue
                sem, inc = si
                for meth, kw in fn:
                    getattr(eng, meth)(**kw).then_inc(sem, inc)
            if final:
                for (ws, wv) in final:
                    eng.wait_ge(ws, wv)

        with nc.Block() as block:
            @block.tensor
            def _(eng):
                run(eng, streams["pe"])

            @block.scalar
            def _(eng):
                run(eng, streams["act"])

            @block.vector
            def _(eng):
                run(eng, streams["dve"])

            @block.gpsimd
            def _(eng):
                run(eng, streams["pool"])

            @block.sync
            def _(eng):
                run(eng, streams["sp"], fin)

    def close(self):
        self.stack.close()


class Arena:
    def __init__(self, S, name, nbytes):
        self.t = S.sbuf(name, [128, nbytes // 4], F32)
        self.nbytes = nbytes
        self.off = 0

    def reset(self, off=0):
        self.off = off

    def view(self, shape, dt, parts=128):
        n = int(np.prod(shape))
        esz = 4 if dt == F32 else 2
        nb = (n * esz + 31) // 32 * 32
        assert self.off + nb <= self.nbytes, (self.off, nb, self.nbytes)
        ap = self.t[0:parts, self.off // 4:(self.off + nb) // 4]
        self.off += nb
        if dt != F32:
            ap = ap.bitcast(dt)
        ap = ap[:, 0:n]
        if len(shape) == 2:
            ap = ap.rearrange("p (a b) -> p a b", a=shape[0])
        elif len(shape) == 3:
            ap = ap.rearrange("p (a b c) -> p a b c", a=shape[0], b=shape[1])
        return ap


def make_consts(S):
    ones = S.sbuf("ones_bf", [128, 128], BF16)
    S.op("dve", "memset", dict(ap=ones[:, :], constant=1.0), writes=["ones"])
    return ones


def rmsnorm_fm(S, xT, xkey, g, gkey, out, okey, ones, sq, rstd, ps, pskey, nchunks=KC, dim=D):
    for h in range(2):
        ts = slice(h * 512, (h + 1) * 512)
        for kc in range(nchunks):
            S.op("act", "activation", dict(out=sq[:, kc, :], in_=xT[:, kc, ts], func=AF.Square),
                 reads=[f"{xkey}{kc}_{h}"], writes=[f"sq{kc}"])
        for kc in range(nchunks):
            S.op("pe", "matmul", dict(out=ps[:, 0:512], lhsT=ones[:, :], rhs=sq[:, kc, :],
                                      start=(kc == 0), stop=(kc == nchunks - 1)),
                 reads=["ones", f"sq{kc}"], writes=[pskey])
        S.op("act", "activation", dict(out=rstd[:, :], in_=ps[:, 0:512], func=AF.Sqrt, bias=EPS, scale=1.0 / dim),
             reads=[pskey], writes=["rstd"])
        S.op("dve", "reciprocal", dict(out=rstd[:, :], in_=rstd[:, :]), reads=["rstd"], writes=["rstd"])
        for kc in range(nchunks):
            S.op("dve", "scalar_tensor_tensor", dict(out=out[:, kc, ts], in0=xT[:, kc, ts], scalar=g[:, kc:kc + 1],
                                                     in1=rstd[:, :], op0=ALU.mult, op1=ALU.mult),
                 reads=[f"{xkey}{kc}_{h}", gkey, "rstd"], writes=[f"{okey}{kc}_{h}"])


def _pieces():
    p = []
    p += [(0 + 128 * i, 128) for i in range(6)]
    p += [(2816 + 96 * i, 96) for i in range(8)]
    p += [(768 + 128 * i, 128) for i in range(2)]
    p += [(1024 + 128 * i, 128) for i in range(2)]
    p += [(1280 + 96 * i, 96) for i in range(4)]
    p += [(1664 + 96 * i, 96) for i in range(4)]
    p += [(2048 + 96 * i, 96) for i in range(8)]
    p += [(3584, 16)]
    p += [(3600 + 128 * i, 128) for i in range(4)]
    p += [(4112 + 128 * i, 128) for i in range(4)]
    p += [(4624 + 128 * i, 128) for i in range(4)]
    return p


PIECES = _pieces()
NPIECE = len(PIECES)
P_AQ, P_BR, P_AK, P_AV, P_BQ, P_BK, P_BV, P_BLR, P_CQ, P_CK, P_CV = 0, 6, 14, 16, 18, 22, 26, 34, 35, 39, 43
X0 = 18
NX = NPIECE - X0
R2 = 160
NIDX = 80


def _groups(maxw=512):
    gs, cur, w = [], [], 0
    for i, (c0, wd) in enumerate(PIECES):
        if cur and (w + wd > maxw or PIECES[cur[-1]][0] + PIECES[cur[-1]][1] != c0):
            gs.append(cur)
            cur, w = [], 0
        cur.append(i)
        w += wd
    gs.append(cur)
    return gs


def emit_phaseA(S, xT, g, gkey, w_d, pj_d, ones, ar, psb, halo_d=None):
    ar.reset(0)
    hT = ar.view([KC, T], BF16)
    sq = ar.view([KC, 512], BF16)
    rstd = ar.view([1, 512], F32)[:, 0, :]
    wgs = [ar.view([KC, 512], BF16) for _ in range(3)]
    stg = [ar.view([1, T], BF16)[:, 0, :] for _ in range(4)]
    rmsnorm_fm(S, xT, "xT", g, gkey, hT, "hT", ones, sq, rstd, psb[7], "ps7")
    groups = _groups()
    wv = w_d.rearrange("(kc p) c -> p kc c", p=128)
    cnt = 0
    for gi, grp in enumerate(groups):
        slot = gi % 3
        wg = wgs[slot]
        c0 = PIECES[grp[0]][0]
        wid = sum(PIECES[i][1] for i in grp)
        S.dma("pool", [dict(out=wg[:, 4 * q:4 * q + 4, 0:wid], in_=wv[:, 4 * q:4 * q + 4, c0:c0 + wid]) for q in range(4)],
              writes=[f"wg{slot}"])
        for pi in grp:
            pc0, pw = PIECES[pi]
            off = pc0 - c0
            sslot = pi % 4
            st = stg[sslot]
            for h in range(2):
                bank = cnt % 4
                cnt += 1
                ps = psb[bank]
                for kc in range(KC):
                    S.op("pe", "matmul", dict(out=ps[0:pw, 0:512], lhsT=wg[:, kc, off:off + pw],
                                              rhs=hT[:, kc, h * 512:(h + 1) * 512], start=(kc == 0), stop=(kc == KC - 1)),
                         reads=[f"wg{slot}", f"hT{kc}_{h}"], writes=[f"ps{bank}"])
                if cnt % 2 == 0:
                    S.op("act", "copy", dict(out=st[0:pw, h * 512:(h + 1) * 512], in_=ps[0:pw, 0:512]),
                         reads=[f"ps{bank}"], writes=[f"stg{sslot}_{h}"])
                else:
                    S.op("dve", "tensor_copy", dict(out=st[0:pw, h * 512:(h + 1) * 512], in_=ps[0:pw, 0:512]),
                         reads=[f"ps{bank}"], writes=[f"stg{sslot}_{h}"])
            kws = [dict(out=pj_d[pi, 0:pw, :], in_=st[0:pw, :])]
            if halo_d is not None and P_AK <= pi < P_AV + 2:
                kws.append(dict(out=halo_d[pi - P_AK, :, :], in_=st[:, T - 128:T]))
            S.dma("sp", kws, reads=[f"stg{sslot}_0", f"stg{sslot}_1"], semkey=f"stgd{sslot}", is_out=True)


def load_xT(S, xT, xT_d):
    xv = xT_d.rearrange("(kc p) t -> p kc t", p=128)
    S.dma("sp", [dict(out=xT[:, kc, :], in_=xv[:, kc, :]) for kc in range(KC)],
          writes=[f"xT{kc}_{h}" for kc in range(KC) for h in range(2)], semkey="xTload")


def build_A():
    nc = bass.Bass("TRN2", target_bir_lowering=False)
    xT_d = nc.dram_tensor("xT", [D, T], F32, kind="ExternalInput").ap()
    g_d = nc.dram_tensor("g", [128, KC], F32, kind="ExternalInput").ap()
    w_d = nc.dram_tensor("w_in", [D, PROJ_W], F32, kind="ExternalInput").ap()
    pj_d = nc.dram_tensor("projT", [NPIECE, 128, T], BF16, kind="ExternalOutput").ap()
    S = Sched(nc)
    xT = S.sbuf("xT_sb", [128, KC, T], F32)
    g = S.sbuf("g_sb", [128, KC], F32)
    psb = [S.psum(f"psb{i}", [128, 512], F32) for i in range(8)]
    ar = Arena(S, "arena", 132 * 1024)
    ones = make_consts(S)
    load_xT(S, xT, xT_d)
    S.dma("sp", dict(out=g[:, :], in_=g_d[:, :]), writes=["g"])
    emit_phaseA(S, xT, g, "g", w_d, pj_d, ones, ar, psb)
    S.emit()
    S.close()
    return nc


def emit_phaseC(S, xT, ones, ar, psb, d, final):
    wo_d = d["w_out"]
    for h in range(2):
        ts = slice(h * 512, (h + 1) * 512)
        ar.reset(0)
        glo = ar.view([8, 512], F32, parts=96)
        brt = ar.view([8, 512], BF16, parts=96)
        sbo = ar.view([4, 512], F32)
        mA = ar.view([6, 512], BF16)
        mB = ar.view([8, 512], BF16, parts=96)
        mC = ar.view([4, 512], BF16)
        sq = ar.view([8, 512], BF16)
        sg = ar.view([8, 512], BF16, parts=96)
        rstd = ar.view([1, 512], F32)[:, 0, :]
        wos = [ar.view([18, 512], BF16) for _ in range(2)]
        S.dma("sp", dict(out=mA[:, :, :], in_=d["aTl"].rearrange("(c p) t -> p c t", p=128)[:, :, ts]), reads=["aTl"], writes=["mA"])
        S.dma("sp", dict(out=brt[:, :, :], in_=d["pjl"][P_BR:P_BR + 8].rearrange("i p t -> p i t")[0:96, :, ts]), reads=["pjl"], writes=["brt"])
        g2 = d["gath2"]
        S.dma("pool", [dict(out=glo[:, i_, :], out_offset=None, in_=g2, in_offset=bass.IndirectOffsetOnAxis(ap=d["idx"][0:96, 56 + 2 * i_ + h:56 + 2 * i_ + h + 1], axis=0)) for i_ in range(NCORES)], reads=["gath2", "idx"], writes=["glo"])
        S.dma("pool", [dict(out=sbo[:, c_, :], out_offset=None, in_=g2, in_offset=bass.IndirectOffsetOnAxis(ap=d["idx"][0:128, 72 + 2 * c_ + h:72 + 2 * c_ + h + 1], axis=0)) for c_ in range(4)], reads=["gath2", "idx"], writes=["sbo"])
        S.op("act", "activation", dict(out=sg[:, :, :], in_=brt[:, :, :], func=AF.Silu), reads=["brt"], writes=["sg"])
        S.op("act", "activation", dict(out=sq[0:96, :, :], in_=glo[:, :, :], func=AF.Square), reads=["glo"], writes=["sq"])
        for hd in range(4):
            ps = psb[hd % 2]
            pk = f"ps{hd % 2}"
            for j in range(2):
                S.op("pe", "matmul", dict(out=ps[0:96, 0:512], lhsT=ones[0:96, 0:96], rhs=sq[0:96, 2 * hd + j, :],
                                          start=(j == 0), stop=(j == 1)),
                     reads=["ones", "sq"], writes=[pk])
            S.op("act", "activation", dict(out=rstd[0:96, :], in_=ps[0:96, 0:512], func=AF.Sqrt, bias=EPS, scale=1.0 / 192),
                 reads=[pk], writes=["rstd"])
            S.op("dve", "reciprocal", dict(out=rstd[0:96, :], in_=rstd[0:96, :]), reads=["rstd"], writes=["rstd"])
            for j in range(2):
                i = 2 * hd + j
                S.op("dve", "scalar_tensor_tensor", dict(out=glo[:, i, :], in0=glo[:, i, :], scalar=d["gn"][0:96, j:j + 1],
                                                         in1=rstd[0:96, :], op0=ALU.mult, op1=ALU.mult),
                     reads=["glo", "gn", "rstd"], writes=["glo"])
        S.op("dve", "tensor_tensor", dict(out=mB[:, :, :], in0=glo[:, :, :], in1=sg[:, :, :], op=ALU.mult),
             reads=["glo", "sg"], writes=["mB"])
        S.op("act", "activation", dict(out=sq[:, 0:4, :], in_=sbo[:, :, :], func=AF.Square), reads=["sbo"], writes=["sq"])
        for c in range(4):
            S.op("pe", "matmul", dict(out=psb[2][:, 0:512], lhsT=ones[:, :], rhs=sq[:, c, :], start=(c == 0), stop=(c == 3)),
                 reads=["ones", "sq"], writes=["ps2"])
        S.op("act", "activation", dict(out=rstd[:, :], in_=psb[2][:, 0:512], func=AF.Sqrt, bias=EPS, scale=1.0 / 512),
             reads=["ps2"], writes=["rstd"])
        S.op("dve", "reciprocal", dict(out=rstd[:, :], in_=rstd[:, :]), reads=["rstd"], writes=["rstd"])
        for c in range(4):
            S.op("dve", "scalar_tensor_tensor", dict(out=mC[:, c, :], in0=sbo[:, c, :], scalar=d["gsb"][:, c:c + 1],
                                                     in1=rstd[:, :], op0=ALU.mult, op1=ALU.mult),
                 reads=["sbo", "gsb", "rstd"], writes=["mC"])
        for og in range(4):
            slot = og % 2
            wo = wos[slot]
            cs = slice(og * 512, (og + 1) * 512)
            S.dma("pool", [
                dict(out=wo[:, 0:6, :], in_=wo_d[0:768, :].rearrange("(c p) n -> p c n", p=128)[:, :, cs]),
                dict(out=wo[0:96, 6:14, :], in_=wo_d[768:1536, :].rearrange("(c p) n -> p c n", p=96)[:, :, cs]),
                dict(out=wo[:, 14:18, :], in_=wo_d[1536:2048, :].rearrange("(c p) n -> p c n", p=128)[:, :, cs])],
                writes=[f"wo{slot}"])
            for dl in range(4):
                dc = og * 4 + dl
                bank = 4 + (dc % 4)
                ps = psb[bank]
                ds_ = slice(dl * 128, (dl + 1) * 128)
                chunks = [(mA[:, c, :], wo[:, c, ds_], "mA") for c in range(6)]
                chunks += [(mB[:, i, :], wo[0:96, 6 + i, ds_], "mB") for i in range(8)]
                chunks += [(mC[:, c, :], wo[:, 14 + c, ds_], "mC") for c in range(4)]
                for ci, (rhs, lhsT, rk) in enumerate(chunks):
                    S.op("pe", "matmul", dict(out=ps[:, 0:512], lhsT=lhsT, rhs=rhs, start=(ci == 0), stop=(ci == 17)),
                         reads=[f"wo{slot}", rk], writes=[f"ps{bank}"])
                S.op("dve", "tensor_tensor", dict(out=xT[:, dc, ts], in0=xT[:, dc, ts], in1=ps[:, 0:512], op=ALU.add),
                     reads=[f"ps{bank}", f"xT{dc}_{h}"], writes=[f"xT{dc}_{h}"])
        if h == 1:
            S.barrier()
    ar.reset(0)
    h2 = ar.view([KC, T], BF16)
    sq = ar.view([KC, 512], BF16)
    rstd = ar.view([1, 512], F32)[:, 0, :]
    rmsnorm_fm(S, xT, "xT", d["gmlp"], "gmlp", h2, "h2", ones, sq, rstd, psb[7], "ps7")
    S.barrier()
    ar.reset(32 * 1024)
    w1s = [ar.view([KC, 512], BF16) for _ in range(2)]
    w2s = [ar.view([4, D], BF16) for _ in range(2)]
    aTs = [ar.view([4, T], BF16) for _ in range(2)]
    rr = [ar.view([1, 512], BF16)[:, 0, :] for _ in range(2)]
    w1v = d["w1"].rearrange("(kc p) f -> p kc f", p=128)
    w2v = d["w2"].rearrange("(fc p) n -> p fc n", p=128)
    NG = DFF // 512
    cnt = 0
    for gi in range(NG):
        slot = gi % 2
        w1, w2, aT = w1s[slot], w2s[slot], aTs[slot]
        fs = slice(gi * 512, (gi + 1) * 512)
        S.dma("pool", [dict(out=w1[:, 4 * q:4 * q + 4, :], in_=w1v[:, 4 * q:4 * q + 4, fs]) for q in range(4)],
              writes=[f"w1_{slot}"])
        S.dma("pool", [dict(out=w2[:, q, :], in_=w2v[:, gi * 4 + q, :]) for q in range(4)], writes=[f"w2_{slot}"])
        for fc in range(4):
            for h in range(2):
                bank = cnt % 4
                rs = cnt % 2
                cnt += 1
                ps = psb[bank]
                hs = slice(h * 512, (h + 1) * 512)
                for kc in range(KC):
                    S.op("pe", "matmul", dict(out=ps[:, 0:512], lhsT=w1[:, kc, fc * 128:(fc + 1) * 128], rhs=h2[:, kc, hs],
                                              start=(kc == 0), stop=(kc == KC - 1)),
                         reads=[f"w1_{slot}", f"h2{kc}_{h}"], writes=[f"ps{bank}"])
                S.op("act", "activation", dict(out=rr[rs][:, :], in_=ps[:, 0:512], func=AF.Relu),
                     reads=[f"ps{bank}"], writes=[f"rr{rs}"])
                S.op("dve", "tensor_tensor", dict(out=aT[:, fc, hs], in0=rr[rs][:, :], in1=rr[rs][:, :], op=ALU.mult),
                     reads=[f"rr{rs}"], writes=[f"aT{slot}_{fc}_{h}"])
        for dc in range(KC):
            for h in range(2):
                bank = 4 + (cnt % 4)
                cnt += 1
                ps = psb[bank]
                hs = slice(h * 512, (h + 1) * 512)
                for fc in range(4):
                    S.op("pe", "matmul", dict(out=ps[:, 0:512], lhsT=w2[:, fc, dc * 128:(dc + 1) * 128], rhs=aT[:, fc, hs],
                                              start=(fc == 0), stop=(fc == 3)),
                         reads=[f"w2_{slot}", f"aT{slot}_{fc}_{h}"], writes=[f"ps{bank}"])
                S.op("dve", "tensor_tensor", dict(out=xT[:, dc, hs], in0=xT[:, dc, hs], in1=ps[:, 0:512], op=ALU.add),
                     reads=[f"ps{bank}", f"xT{dc}_{h}"], writes=[f"xT{dc}_{h}"])
    S.barrier()
    if final:
        ar.reset(0)
        sq = ar.view([KC, 512], BF16)
        rstd = ar.view([1, 512], F32)[:, 0, :]
        rmsnorm_fm(S, xT, "xT", d["gfin"], "gfin", xT, "xT", ones, sq, rstd, psb[7], "ps7")


def build_C(final):
    nc = bass.Bass("TRN2", target_bir_lowering=False)
    dd = {}
    xT_d = nc.dram_tensor("xT", [D, T], F32, kind="ExternalInput").ap()
    dd["aT"] = nc.dram_tensor("aT", [768, T], BF16, kind="ExternalInput").ap()
    dd["glo"] = nc.dram_tensor("glo", [8, 96, T], F32, kind="ExternalInput").ap()
    dd["brT"] = nc.dram_tensor("brT", [8, 96, T], BF16, kind="ExternalInput").ap()
    dd["sbo"] = nc.dram_tensor("sbo", [8, 64, T], F32, kind="ExternalInput").ap()
    dd["w_out"] = nc.dram_tensor("w_out", [D, D], F32, kind="ExternalInput").ap()
    dd["w1"] = nc.dram_tensor("w1", [D, DFF], F32, kind="ExternalInput").ap()
    dd["w2"] = nc.dram_tensor("w2", [DFF, D], F32, kind="ExternalInput").ap()
    prm_d = nc.dram_tensor("prm", [128, 64], F32, kind="ExternalInput").ap()
    xo_d = nc.dram_tensor("xoT", [D, T], F32, kind="ExternalOutput").ap()
    S = Sched(nc)
    xT = S.sbuf("xT_sb", [128, KC, T], F32)
    prm = S.sbuf("prm_sb", [128, 64], F32)
    psb = [S.psum(f"psb{i}", [128, 512], F32) for i in range(8)]
    ar = Arena(S, "arena", 132 * 1024)
    ones = make_consts(S)
    load_xT(S, xT, xT_d)
    S.dma("sp", dict(out=prm[:, :], in_=prm_d[:, :]), writes=["gn", "gsb", "gmlp", "gfin"], semkey="prm")
    dd["gmlp"] = prm[:, 0:16]
    dd["gfin"] = prm[:, 16:32]
    dd["gsb"] = prm[:, 32:36]
    dd["gn"] = prm[:, 36:38]
    emit_phaseC(S, xT, ones, ar, psb, dd, final)
    xov = xo_d.rearrange("(kc p) t -> p kc t", p=128)
    S.dma("sp", [dict(out=xov[:, kc, :], in_=xT[:, kc, :]) for kc in range(KC)],
          reads=[f"xT{kc}_{h}" for kc in range(KC) for h in range(2)], semkey="xTstore", is_out=True)
    S.emit()
    S.close()
    return nc


def make_ident(S):
    ident = S.sbuf("ident_bf", [128, 128], BF16)
    S.op("pool", "memset", dict(ap=ident[:, :], constant=1.0), writes=["ident"])
    S.op("pool", "affine_select", dict(out=ident[:, :], in_=ident[:, :], pattern=[[-1, 128]], compare_op=ALU.is_equal,
                                       fill=0.0, base=0, channel_multiplier=1), reads=["ident"], writes=["ident"])
    return ident


def tri_const(S, name, dt, val, op, base=0):
    t = S.sbuf(name, [128, 128], dt)
    S.op("pool", "memset", dict(ap=t[:, :], constant=val), writes=[name])
    S.op("pool", "affine_select", dict(out=t[:, :], in_=t[:, :], pattern=[[-1, 128]], compare_op=op,
                                       fill=0.0, base=base, channel_multiplier=1), reads=[name], writes=[name])
    return t


def emit_swa(S, ar, psb, d, ident):
    ar.reset(0)
    q = ar.view([12, T], BF16, parts=64)
    k = ar.view([4, T + 128], BF16, parts=64)
    v = ar.view([4, T + 128], BF16, parts=64)
    bias = ar.view([12, 256], F32)
    bias0 = ar.view([12, 256], F32)
    gA = ar.view([1, 768], F32)[:, 0, :]
    esink = ar.view([1, 12], F32)[:, 0, :]
    Vt = ar.view([9, 4, 65], BF16)
    lg = [ar.view([1, 256], F32)[:, 0, :] for _ in range(3)]
    pp = [ar.view([1, 256], BF16)[:, 0, :] for _ in range(3)]
    den = ar.view([1, 12], F32)[:, 0, :]
    oa = ar.view([12, 64], F32)
    oan = ar.view([1, 768], BF16)[:, 0, :]
    junk = ar.view([1, 768], F32)[:, 0, :]
    ss = ar.view([1, 2], F32)[:, 0, :]
    aTs = ar.view([6, T], BF16)
    pjl = d["pjl"]
    S.dma("sp", dict(out=q[:, :, :], in_=pjl[P_AQ:P_AQ + 6].rearrange("c (two p) t -> p (c two) t", two=2)), reads=["pjl"], writes=["swq"])
    S.dma("sp", dict(out=k[:, :, 128:], in_=pjl[P_AK:P_AK + 2].rearrange("c (two p) t -> p (c two) t", two=2)), reads=["pjl"], writes=["swk"])
    S.dma("sp", dict(out=v[:, :, 128:], in_=pjl[P_AV:P_AV + 2].rearrange("c (two p) t -> p (c two) t", two=2)), reads=["pjl"], writes=["swv"])
    S.dma("pool", [dict(out=k[:, g, 0:128], out_offset=None, in_=d["gathh"], in_offset=bass.IndirectOffsetOnAxis(ap=d["idx"][0:64, 48 + g:48 + g + 1], axis=0)) for g in range(4)], reads=["gathh", "idx"], writes=["swk"], semkey="swkh")
    S.dma("pool", [dict(out=v[:, g, 0:128], out_offset=None, in_=d["gathh"], in_offset=bass.IndirectOffsetOnAxis(ap=d["idx"][0:64, 52 + g:52 + g + 1], axis=0)) for g in range(4)], reads=["gathh", "idx"], writes=["swv"], semkey="swvh")
    S.dma("sp", dict(out=bias[:, :, :], in_=d["swb"]), writes=["swb"])
    S.dma("sp", dict(out=bias0[:, :, :], in_=d["swb0"]), writes=["swb0"])
    S.dma("sp", dict(out=gA[:, :], in_=d["gA"]), writes=["gA"])
    S.dma("sp", dict(out=esink[:, :], in_=d["sinks"]), writes=["esink"])
    S.op("act", "activation", dict(out=esink[:, :], in_=esink[:, :], func=AF.Exp), reads=["esink"], writes=["esink"])
    S.op("pool", "memset", dict(ap=Vt[:, :, :, :], constant=1.0), writes=["Vt"])
    for b in range(9):
        ps = psb[b % 2]
        for g in range(4):
            S.op("pe", "matmul", dict(out=ps[:, 64 * g:64 * g + 64], lhsT=v[:, g, 128 * b:128 * b + 128], rhs=ident[0:64, 0:64],
                                      start=True, stop=True), reads=["swv", "ident"], writes=[f"ps{b % 2}"])
        S.op("act", "copy", dict(out=Vt[:, b, :, 0:64], in_=ps[:, 0:256].rearrange("p (g e) -> p g e", g=4)),
             reads=[f"ps{b % 2}"], writes=["Vt"])
    cnt = 0
    for b in range(8):
        bt = bias0 if b == 0 else bias
        bk = "swb0" if b == 0 else "swb"
        for h in range(12):
            g = h // 3
            sb = 2 + (cnt % 2)
            ls = cnt % 3
            cnt += 1
            ps = psb[sb]
            S.op("pe", "matmul", dict(out=ps[:, 0:128], lhsT=k[:, g, 128 * b:128 * b + 128], rhs=q[:, h, 128 * b:128 * b + 128],
                                      start=True, stop=True), reads=["swk", "swq"], writes=[f"ps{sb}"])
            S.op("pe", "matmul", dict(out=ps[:, 128:256], lhsT=k[:, g, 128 * b + 128:128 * b + 256], rhs=q[:, h, 128 * b:128 * b + 128],
                                      start=True, stop=True), reads=["swk", "swq"], writes=[f"ps{sb}"])
            S.op("dve", "scalar_tensor_tensor", dict(out=lg[ls][:, :], in0=ps[:, 0:256], scalar=0.125, in1=bt[:, h, :],
                                                     op0=ALU.mult, op1=ALU.add), reads=[f"ps{sb}", bk], writes=[f"lg{ls}"])
            S.op("act", "activation", dict(out=pp[ls][:, :], in_=lg[ls][:, :], func=AF.Exp), reads=[f"lg{ls}"], writes=[f"pp{ls}"])
            ob = 4 + (h // 6)
            oc = (h % 6) * 65
            S.op("pe", "matmul", dict(out=psb[ob][:, oc:oc + 65], lhsT=pp[ls][:, 0:128], rhs=Vt[:, b, g, :], start=True, stop=False),
                 reads=[f"pp{ls}", "Vt"], writes=[f"ps{ob}"])
            S.op("pe", "matmul", dict(out=psb[ob][:, oc:oc + 65], lhsT=pp[ls][:, 128:256], rhs=Vt[:, b + 1, g, :], start=False, stop=True),
                 reads=[f"pp{ls}", "Vt"], writes=[f"ps{ob}"])
        for hb in range(2):
            pv = psb[4 + hb][:, 0:390].rearrange("p (h e) -> p h e", h=6)
            S.op("dve", "tensor_tensor", dict(out=den[:, 6 * hb:6 * hb + 6], in0=pv[:, :, 64], in1=esink[:, 6 * hb:6 * hb + 6], op=ALU.add),
                 reads=[f"ps{4 + hb}", "esink"], writes=["den"])
        S.op("dve", "reciprocal", dict(out=den[:, :], in_=den[:, :]), reads=["den"], writes=["den"])
        for h in range(12):
            pv = psb[4 + h // 6][:, 0:390].rearrange("p (h e) -> p h e", h=6)
            S.op("dve", "tensor_scalar", dict(out=oa[:, h, :], in0=pv[:, h % 6, 0:64], scalar1=den[:, h:h + 1], scalar2=None, op0=ALU.mult),
                 reads=[f"ps{4 + h // 6}", "den"], writes=["oa"])
        oaf = oa.rearrange("p h e -> p (h e)")
        S.op("act", "activation", dict(out=junk[:, :], in_=oaf, func=AF.Square, accum_out=ss[:, 0:1]), reads=["oa"], writes=["junk", "ss"])
        S.op("act", "activation", dict(out=ss[:, 1:2], in_=ss[:, 0:1], func=AF.Sqrt, bias=EPS, scale=1.0 / 768), reads=["ss"], writes=["ss"])
        S.op("dve", "reciprocal", dict(out=ss[:, 1:2], in_=ss[:, 1:2]), reads=["ss"], writes=["ss"])
        S.op("dve", "scalar_tensor_tensor", dict(out=oan[:, :], in0=oaf, scalar=ss[:, 1:2], in1=gA[:, :], op0=ALU.mult, op1=ALU.mult),
             reads=["oa", "ss", "gA"], writes=["oan"])
        for c in range(6):
            tb = 6 + (c // 4)
            S.op("pe", "matmul", dict(out=psb[tb][:, 128 * (c % 4):128 * (c % 4) + 128], lhsT=oan[:, 128 * c:128 * c + 128], rhs=ident[:, :],
                                      start=True, stop=True), reads=["oan", "ident"], writes=[f"ps{tb}"])
        S.op("act", "copy", dict(out=aTs[:, 0:4, 128 * b:128 * b + 128], in_=psb[6][:, 0:512].rearrange("p (c t) -> p c t", c=4)),
             reads=["ps6"], writes=["aTs"])
        S.op("act", "copy", dict(out=aTs[:, 4:6, 128 * b:128 * b + 128], in_=psb[7][:, 0:256].rearrange("p (c t) -> p c t", c=2)),
             reads=["ps7"], writes=["aTs"])
    S.dma("sp", dict(out=d["aTl"].rearrange("(c p) t -> p c t", p=128), in_=aTs[:, :, :]), reads=["aTs"], writes=["aTl"], semkey="aTst", is_out=True)


GLA_DBG = {"nt": None, "stage": 9}


def emit_gla(S, ar, psb, d, ident, consts):
    tri_in, tri_cmp, msk = consts
    NT = SEQ // 128
    ar.reset(0)
    gq = ar.view([1, SEQ], BF16, parts=96)[:, 0, :]
    gk = ar.view([1, SEQ], BF16, parts=96)[:, 0, :]
    gv = ar.view([1, SEQ], BF16, parts=96)[:, 0, :]
    blr = ar.view([1, SEQ], BF16, parts=32)[:, 0, :]
    gw = ar.view([1, 96], BF16, parts=32)[:, 0, :]
    St = ar.view([1, 96], F32, parts=96)[:, 0, :]
    S16 = [ar.view([1, 96], BF16, parts=96)[:, 0, :] for _ in range(2)]
    NB = 3
    e1 = [ar.view([1, 96], F32)[:, 0, :] for _ in range(NB)]
    lgt = [ar.view([1, 96], F32)[:, 0, :] for _ in range(NB)]
    lgh = [ar.view([1, 96], BF16)[:, 0, :] for _ in range(NB)]
    lgl = [ar.view([1, 96], BF16)[:, 0, :] for _ in range(NB)]
    ebT = [ar.view([1, 128], F32, parts=96)[:, 0, :] for _ in range(NB)]
    einvT = [ar.view([1, 128], F32, parts=96)[:, 0, :] for _ in range(NB)]
    erem = [ar.view([1, 96], F32)[:, 0, :] for _ in range(NB)]
    qdT = [ar.view([1, 128], BF16, parts=96)[:, 0, :] for _ in range(NB)]
    kiT = [ar.view([1, 128], BF16, parts=96)[:, 0, :] for _ in range(NB)]
    kend = [ar.view([1, 96], BF16)[:, 0, :] for _ in range(NB)]
    vt = [ar.view([1, 96], BF16)[:, 0, :] for _ in range(NB)]
    sc = [ar.view([1, 128], BF16)[:, 0, :] for _ in range(NB)]
    ost = [ar.view([1, T], F32, parts=96)[:, 0, :] for _ in range(2)]
    g1 = d["gath1"]
    S.dma("pool", [dict(out=gq[:, 1024 * s_:1024 * s_ + 1024], out_offset=None, in_=g1, in_offset=bass.IndirectOffsetOnAxis(ap=d["idx"][0:96, 24 + s_:24 + s_ + 1], axis=0)) for s_ in range(NCORES)], reads=["gath1", "idx"], writes=["gq"])
    S.dma("pool", [dict(out=gk[:, 1024 * s_:1024 * s_ + 1024], out_offset=None, in_=g1, in_offset=bass.IndirectOffsetOnAxis(ap=d["idx"][0:96, 32 + s_:32 + s_ + 1], axis=0)) for s_ in range(NCORES)], reads=["gath1", "idx"], writes=["gk"])
    S.dma("pool", [dict(out=gv[:, 1024 * s_:1024 * s_ + 1024], out_offset=None, in_=g1, in_offset=bass.IndirectOffsetOnAxis(ap=d["idx"][0:96, 40 + s_:40 + s_ + 1], axis=0)) for s_ in range(NCORES)], reads=["gath1", "idx"], writes=["gv"])
    S.op("pool", "memset", dict(ap=blr[:, :], constant=1.0), writes=["blr"])
    S.dma("sp", dict(out=blr[0:16, :].rearrange("p (s t) -> p s t", s=NCORES),
                     in_=g1.rearrange("(s q r) t -> r s q t", s=NCORES, q=NX)[0:16, :, P_BLR - X0, :]), reads=["gath1"], writes=["blr"])
    S.dma("pool", dict(out=gw[0:17, :], in_=d["gw"]), writes=["gw"])
    S.op("dve", "memset", dict(ap=St[:, :], constant=0.0), writes=["St"])
    S.op("dve", "memset", dict(ap=S16[0][:, :], constant=0.0), writes=["S16_0"])
    sidx = 0
    if GLA_DBG["nt"]:
        NT = GLA_DBG["nt"]
    stg_ = GLA_DBG["stage"]
    def part1(i):
        r = i % NB
        tk = slice(128 * i, 128 * i + 128)
        rk = f"_{r}"
        S.op("pe", "matmul", dict(out=psb[0][:, 0:96], lhsT=blr[0:17, tk], rhs=gw[0:17, :], start=True, stop=True),
             reads=["blr", "gw"], writes=["ps0"])
        S.op("act", "activation", dict(out=e1[r][:, :], in_=psb[0][:, 0:96], func=AF.Exp, scale=-1.0), reads=["ps0"], writes=["e1" + rk])
        S.op("act", "activation", dict(out=e1[r][:, :], in_=e1[r][:, :], func=AF.Ln, bias=1.0), reads=["e1" + rk], writes=["e1" + rk])
        S.op("dve", "tensor_scalar", dict(out=lgt[r][:, :], in0=e1[r][:, :], scalar1=-1.0 / 16.0, scalar2=-1.0, op0=ALU.mult, op1=ALU.max),
             reads=["e1" + rk], writes=["lgt" + rk])
        if stg_ < 2:
            return
        S.op("dve", "tensor_copy", dict(out=lgh[r][:, :], in_=lgt[r][:, :]), reads=["lgt" + rk], writes=["lgh" + rk])
        S.op("dve", "tensor_tensor", dict(out=lgl[r][:, :], in0=lgt[r][:, :], in1=lgh[r][:, :], op=ALU.subtract),
             reads=["lgt" + rk, "lgh" + rk], writes=["lgl" + rk])
        for hl, (lg_, lk) in enumerate(((lgh, "lgh"), (lgl, "lgl"))):
            S.op("pe", "matmul", dict(out=psb[1][0:96, 0:128], lhsT=lg_[r][:, :], rhs=tri_in[:, :], start=(hl == 0), stop=(hl == 1)),
                 reads=[lk + rk, "tri_in"], writes=["ps1"])
        for hl, (lg_, lk) in enumerate(((lgh, "lgh"), (lgl, "lgl"))):
            S.op("pe", "matmul", dict(out=psb[0][:, 96:192], lhsT=tri_cmp[:, :], rhs=lg_[r][:, :], start=(hl == 0), stop=(hl == 1)),
                 reads=[lk + rk, "tri_cmp"], writes=["ps0"])
        S.op("act", "activation", dict(out=ebT[r][:, :], in_=psb[1][0:96, 0:128], func=AF.Exp), reads=["ps1"], writes=["ebT" + rk])
        S.op("act", "activation", dict(out=einvT[r][:, :], in_=psb[1][0:96, 0:128], func=AF.Exp, scale=-1.0), reads=["ps1"], writes=["einvT" + rk])
        S.op("act", "activation", dict(out=erem[r][:, :], in_=psb[0][:, 96:192], func=AF.Exp), reads=["ps0"], writes=["erem" + rk])
        if stg_ < 3:
            return
        S.op("dve", "scalar_tensor_tensor", dict(out=qdT[r][:, :], in0=gq[:, tk], scalar=96.0 ** -0.5, in1=ebT[r][:, :],
                                                 op0=ALU.mult, op1=ALU.mult), reads=["gq", "ebT" + rk], writes=["qdT" + rk])
        S.op("dve", "tensor_tensor", dict(out=kiT[r][:, :], in0=gk[:, tk], in1=einvT[r][:, :], op=ALU.mult),
             reads=["gk", "einvT" + rk], writes=["kiT" + rk])
        S.op("pe", "matmul", dict(out=psb[3][:, 0:96], lhsT=gk[:, tk], rhs=ident[0:96, 0:96], start=True, stop=True),
             reads=["gk", "ident"], writes=["ps3"])
        S.op("pe", "matmul", dict(out=psb[3][:, 96:192], lhsT=gv[:, tk], rhs=ident[0:96, 0:96], start=True, stop=True),
             reads=["gv", "ident"], writes=["ps3"])
        S.op("dve", "tensor_tensor", dict(out=kend[r][:, :], in0=psb[3][:, 0:96], in1=erem[r][:, :], op=ALU.mult),
             reads=["ps3", "erem" + rk], writes=["kend" + rk])
        S.op("dve", "tensor_copy", dict(out=vt[r][:, :], in_=psb[3][:, 96:192]), reads=["ps3"], writes=["vt" + rk])
        if stg_ < 4:
            return
        S.op("pe", "matmul", dict(out=psb[3][:, 192:320], lhsT=kiT[r][:, :], rhs=qdT[r][:, :], start=True, stop=True),
             reads=["kiT" + rk, "qdT" + rk], writes=["ps3"])
        S.op("dve", "tensor_tensor", dict(out=sc[r][:, :], in0=psb[3][:, 192:320], in1=msk[:, :], op=ALU.mult),
             reads=["ps3", "msk"], writes=["sc" + rk])
        for c in range(2):
            db = ((5, 6), (2, 4))[i % 2][c]
            S.op("pe", "matmul", dict(out=psb[db][0:96, 0:96], lhsT=kend[r][64 * c:64 * c + 64, :], rhs=vt[r][64 * c:64 * c + 64, :],
                                      start=True, stop=True), reads=["kend" + rk, "vt" + rk], writes=[f"ps{db}"])

    def part2(i):
        nonlocal sidx
        r = i % NB
        rk = f"_{r}"
        C = psb[7]
        S.op("pe", "matmul", dict(out=C[0:96, 0:128], lhsT=vt[r][:, :], rhs=sc[r][:, :], start=True, stop=False),
             reads=["vt" + rk, "sc" + rk], writes=["ps7"])
        for c in range(2):
            S.op("pe", "matmul", dict(out=C[0:96, 64 * c:64 * c + 64], lhsT=S16[sidx][:, :], rhs=qdT[r][:, 64 * c:64 * c + 64],
                                      start=False, stop=(c == 1)), reads=[f"S16_{sidx}", "qdT" + rk], writes=["ps7"])
            db = ((5, 6), (2, 4))[i % 2][c]
            S.op("dve", "scalar_tensor_tensor", dict(out=St[:, :], in0=St[:, :], scalar=ebT[r][:, 64 * c + 63:64 * c + 64],
                                                     in1=psb[db][0:96, 0:96], op0=ALU.mult, op1=ALU.add),
                 reads=["St", "ebT" + rk, f"ps{db}"], writes=["St"])
            sidx ^= 1
            S.op("act", "copy", dict(out=S16[sidx][:, :], in_=St[:, :]), reads=["St"], writes=[f"S16_{sidx}"])
        oslot = (i // 8) % 2
        S.op("act", "copy", dict(out=ost[oslot][:, 128 * (i % 8):128 * (i % 8) + 128], in_=C[0:96, 0:128]),
             reads=["ps7"], writes=[f"ost{oslot}"])
        if i % 8 == 7:
            s2v = d["send2"].rearrange("(j h r) t -> j h r t", j=NCORES, h=2)
            S.dma("sp", [dict(out=s2v[i // 8, h_, 64:160, :], in_=ost[oslot][:, 512 * h_:512 * h_ + 512]) for h_ in range(2)],
                  reads=[f"ost{oslot}"], semkey=f"ostd{oslot}", is_out=True)
    for i in range(NT + 1):
        if i < NT:
            part1(i)
        if i >= 1 and stg_ >= 5:
            part2(i - 1)


def emit_sb(S, ar, psb, d, ident, consts):
    triN, cmpN, mfull = consts
    ar.reset(0)
    qT = ar.view([1, SEQ], BF16, parts=64)[:, 0, :]
    kT = ar.view([1, SEQ], BF16, parts=64)[:, 0, :]
    vT = ar.view([1, SEQ], BF16, parts=64)[:, 0, :]
    Vt = ar.view([64, 64], BF16)
    NBUF = 3
    st = {}
    for s_ in range(2):
        st[s_] = dict(
            e=[ar.view([1, 512], F32)[:, 0, :] for _ in range(NBUF)],
            sp=[ar.view([1, 512], BF16)[:, 0, :] for _ in range(NBUF)],
            xx=[ar.view([1, 512], F32)[:, 0, :] for _ in range(2)],
            w=[ar.view([1, 512], BF16)[:, 0, :] for _ in range(NBUF)],
            o=ar.view([1, 512], F32, parts=64)[:, 0, :],
        )
    g1 = d["gath1"]
    S.dma("pool", [dict(out=qT[:, 1024 * s_:1024 * s_ + 1024], out_offset=None, in_=g1, in_offset=bass.IndirectOffsetOnAxis(ap=d["idx"][0:64, 0 + s_:0 + s_ + 1], axis=0)) for s_ in range(NCORES)], reads=["gath1", "idx"], writes=["sbq"])
    S.dma("pool", [dict(out=kT[:, 1024 * s_:1024 * s_ + 1024], out_offset=None, in_=g1, in_offset=bass.IndirectOffsetOnAxis(ap=d["idx"][0:64, 8 + s_:8 + s_ + 1], axis=0)) for s_ in range(NCORES)], reads=["gath1", "idx"], writes=["sbk"])
    S.dma("pool", [dict(out=vT[:, 1024 * s_:1024 * s_ + 1024], out_offset=None, in_=g1, in_offset=bass.IndirectOffsetOnAxis(ap=d["idx"][0:64, 16 + s_:16 + s_ + 1], axis=0)) for s_ in range(NCORES)], reads=["gath1", "idx"], writes=["sbv"])
    for kb8 in range(8):
        ps = psb[kb8 % 2]
        for j in range(8):
            kb = kb8 * 8 + j
            S.op("pe", "matmul", dict(out=ps[:, 64 * j:64 * j + 64], lhsT=vT[:, 128 * kb:128 * kb + 128], rhs=ident[0:64, 0:64],
                                      start=True, stop=True), reads=["sbv", "ident"], writes=[f"ps{kb8 % 2}"])
        S.op("act", "copy", dict(out=Vt[:, 8 * kb8:8 * kb8 + 8, :], in_=ps[:, 0:512].rearrange("p (j e) -> p j e", j=8)),
             reads=[f"ps{kb8 % 2}"], writes=["Vt"])
    S.barrier()
    NG = SEQ // 512

    def pairs(g):
        return [(g, kb) for kb in range(4 * g + 3, -1, -1)]

    order = [[], []]
    lo, hi = 0, NG - 1
    tgl = 0
    while lo <= hi:
        order[tgl] += pairs(hi)
        order[1 - tgl] += pairs(lo) if lo != hi else []
        lo += 1
        hi -= 1
        tgl ^= 1
    nsteps = max(len(order[0]), len(order[1]))

    def stage1(s_, idx):
        g, kb = order[s_][idx]
        b = idx % NBUF
        T_ = st[s_]
        bank = 4 * s_ + (idx % 2)
        ps = psb[bank]
        S.op("pe", "matmul", dict(out=ps[:, 0:512], lhsT=kT[:, 128 * kb:128 * kb + 128], rhs=qT[:, 512 * g:512 * g + 512],
                                  start=True, stop=True), reads=["sbk", "sbq"], writes=[f"ps{bank}"])
        S.op("act", "activation", dict(out=T_["e"][b][:, :], in_=ps[:, 0:512], func=AF.Exp, scale=0.125),
             reads=[f"ps{bank}"], writes=[f"e{s_}_{b}"])
        S.op("act", "activation", dict(out=T_["sp"][b][:, :], in_=T_["e"][b][:, :], func=AF.Ln, bias=1.0),
             reads=[f"e{s_}_{b}"], writes=[f"sp{s_}_{b}"])
        r = kb - 4 * g
        if r >= 0:
            S.op("pool", "tensor_tensor", dict(out=T_["sp"][b][:, :], in0=T_["sp"][b][:, :], in1=mfull[r][:, :], op=ALU.mult),
                 reads=[f"sp{s_}_{b}", "mfull"], writes=[f"sp{s_}_{b}"])
            S.op("pool", "tensor_tensor", dict(out=T_["e"][b][:, :], in0=T_["e"][b][:, :], in1=mfull[r][:, :], op=ALU.mult),
                 reads=[f"e{s_}_{b}", "mfull"], writes=[f"e{s_}_{b}"])

    def stage2(s_, idx):
        g, kb = order[s_][idx]
        b = idx % NBUF
        T_ = st[s_]
        bank = 4 * s_ + 2
        ps = psb[bank]
        first = (kb == 4 * g + 3)
        if not first:
            pb = (idx - 1) % NBUF
            S.op("pe", "matmul", dict(out=ps[:, 0:512], lhsT=cmpN[:, :], rhs=T_["sp"][pb][:, :], start=False, stop=False, skip_group_check=True),
                 reads=["cmpN", f"sp{s_}_{pb}"], writes=[f"ps{bank}"])
        S.op("pe", "matmul", dict(out=ps[:, 0:512], lhsT=triN[:, :], rhs=T_["sp"][b][:, :], start=first, stop=True, skip_group_check=True),
             reads=["triN", f"sp{s_}_{b}"], writes=[f"ps{bank}"])
        xb = idx % 2
        S.op("act", "activation", dict(out=T_["xx"][xb][:, :], in_=ps[:, 0:512], func=AF.Exp),
             reads=[f"ps{bank}"], writes=[f"xx{s_}_{xb}"])
        S.op("dve", "tensor_tensor", dict(out=T_["w"][b][:, :], in0=T_["e"][b][:, :], in1=T_["xx"][xb][:, :], op=ALU.mult),
             reads=[f"e{s_}_{b}", f"xx{s_}_{xb}"], writes=[f"w{s_}_{b}"])

    def stage3(s_, idx):
        g, kb = order[s_][idx]
        b = idx % NBUF
        T_ = st[s_]
        bank = 4 * s_ + 3
        ps = psb[bank]
        first = (kb == 4 * g + 3)
        S.op("pe", "matmul", dict(out=ps[0:64, 0:512], lhsT=Vt[:, kb, :], rhs=T_["w"][b][:, :], start=first, stop=(kb == 0)),
             reads=["Vt", f"w{s_}_{b}"], writes=[f"ps{bank}"])
        if kb == 0:
            S.op("act", "copy", dict(out=T_["o"][:, :], in_=ps[0:64, 0:512]), reads=[f"ps{bank}"], writes=[f"o{s_}"])
            s2v = d["send2"].rearrange("(j h r) t -> j h r t", j=NCORES, h=2)
            S.dma("sp", dict(out=s2v[g // 2, g % 2, 0:64, :], in_=T_["o"][:, :]), reads=[f"o{s_}"], semkey=f"od{s_}", is_out=True)

    for step in range(nsteps + 2):
        for s_ in range(2):
            if step < len(order[s_]):
                stage1(s_, step)
        for s_ in range(2):
            if 0 <= step - 1 < len(order[s_]):
                stage2(s_, step - 1)
        for s_ in range(2):
            if 0 <= step - 2 < len(order[s_]):
                stage3(s_, step - 2)


def mixer_consts(S):
    ident = make_ident(S)
    tri_in = S.sbuf("tri_in", [128, 128], BF16)
    tri_cmp = S.sbuf("tri_cmp", [128, 128], BF16)
    msk = S.sbuf("msk_bf", [128, 128], BF16)
    for t_, nm, (cm, pj_, bs) in ((tri_in, "tri_in", (-1, 1, 0)), (tri_cmp, "tri_cmp", (1, -1, -1)), (msk, "msk", (-1, 1, 0))):
        S.op("pool", "memset", dict(ap=t_[:, :], constant=1.0), writes=[nm])
        S.op("pool", "affine_select", dict(out=t_[:, :], in_=t_[:, :], pattern=[[pj_, 128]], compare_op=ALU.is_ge, fill=0.0, base=bs,
                                           channel_multiplier=cm), reads=[nm], writes=[nm])
        S.op("pool", "memset", dict(ap=t_[0:64, 64:128], constant=0.0), reads=[nm], writes=[nm])
        S.op("pool", "memset", dict(ap=t_[64:128, 0:64], constant=0.0), reads=[nm], writes=[nm])
    triN = S.sbuf("triN", [128, 128], BF16)
    cmpN = S.sbuf("cmpN", [128, 128], BF16)
    for t_, nm, (cm, pj_, bs) in ((triN, "triN", (1, -1, 0)), (cmpN, "cmpN", (-1, 1, -1))):
        S.op("pool", "memset", dict(ap=t_[:, :], constant=-1.0), writes=[nm])
        S.op("pool", "affine_select", dict(out=t_[:, :], in_=t_[:, :], pattern=[[pj_, 128]], compare_op=ALU.is_ge, fill=0.0, base=bs,
                                           channel_multiplier=cm), reads=[nm], writes=[nm])
    mfull = []
    for r in range(4):
        m = S.sbuf(f"mfull{r}", [128, 512], F32)
        S.op("pool", "memset", dict(ap=m[:, :], constant=1.0), writes=["mfull"])
        S.op("pool", "affine_select", dict(out=m[:, :], in_=m[:, :], pattern=[[1, 512]], compare_op=ALU.is_ge, fill=0.0, base=-128 * r - 1,
                                           channel_multiplier=-1), reads=["mfull"], writes=["mfull"])
        mfull.append(m)
    return ident, (tri_in, tri_cmp, msk), (triN, cmpN, mfull)


DEPTH = 2
I32 = mybir.dt.int32


def build_fused():
    nc = bass.Bass("TRN2", target_bir_lowering=False)

    def din(name, shape, dt):
        return nc.dram_tensor(name, list(shape), dt, kind="ExternalInput").ap()

    xT_d = din("xT", [D, T], F32)
    w_in = din("w_in", [DEPTH, D, PROJ_W], F32)
    w_out = din("w_out", [DEPTH, D, D], F32)
    w1 = din("w1", [DEPTH, D, DFF], F32)
    w2 = din("w2", [DEPTH, DFF, D], F32)
    prm_d = din("prm", [128, 160], F32)
    gw_d = din("gw", [DEPTH, 17, 96], F32)
    swb_d = din("swb", [128, 12, 256], F32)
    swb0_d = din("swb0", [128, 12, 256], F32)
    gA_d = din("gA", [DEPTH, 128, 768], F32)
    sinks_d = din("sinks", [DEPTH, 128, 12], F32)
    idx_d = din("idx", [128, NIDX], I32)
    xo_d = nc.dram_tensor("xoT", [D, T], F32, kind="ExternalOutput").ap()
    pjl = nc.dram_tensor("pjl", [NPIECE, 128, T], BF16).ap()
    hsend = nc.dram_tensor("hsend", [4, 128, 128], BF16).ap()
    gathh = nc.dram_tensor("gathh", [NCORES * 4 * 128, 128], BF16).ap()
    gath1 = nc.dram_tensor("gath1", [NCORES * NX * 128, T], BF16).ap()
    send2 = nc.dram_tensor("send2", [NCORES * 2 * R2, 512], F32).ap()
    gath2 = nc.dram_tensor("gath2", [NCORES * NCORES * 2 * R2, 512], F32).ap()
    aTl = nc.dram_tensor("aTl", [768, T], BF16).ap()

    S = Sched(nc)
    xT = S.sbuf("xT_sb", [128, KC, T], F32)
    prm = S.sbuf("prm_sb", [128, 160], F32)
    idx = S.sbuf("idx_sb", [128, NIDX], I32)
    psb = [S.psum(f"psb{i}", [128, 512], F32) for i in range(8)]
    ar = Arena(S, "arena", 116 * 1024)
    ones = make_consts(S)
    ident, gla_c, sb_c = mixer_consts(S)
    load_xT(S, xT, xT_d)
    S.dma("sp", dict(out=prm[:, :], in_=prm_d[:, :]), writes=["prm", "gn", "gsb", "gmlp", "gfin"], semkey="prm")
    S.dma("sp", dict(out=idx[:, :], in_=idx_d[:, :]), writes=["idx"])
    for l in range(DEPTH):
        pb = 64 * l
        emit_phaseA(S, xT, prm[:, pb:pb + 16], "prm", w_in[l], pjl, ones, ar, psb, halo_d=hsend)
        S.barrier()
        S.cc("AllGather", [hsend.rearrange("q r t -> (q r) t")], [gathh], writes=["gathh"], semkey="cch")
        S.cc("AllGather", [pjl.rearrange("q r t -> (q r) t")[X0 * 128:NPIECE * 128, :]], [gath1], reads=["gathh"], writes=["gath1"], semkey="cc1")
        dB = dict(pjl=pjl, gathh=gathh, gath1=gath1, send2=send2, aTl=aTl, idx=idx, swb=swb_d, swb0=swb0_d,
                  gA=gA_d[l], sinks=sinks_d[l], gw=gw_d[l])
        emit_swa(S, ar, psb, dB, ident)
        S.barrier()
        emit_gla(S, ar, psb, dB, ident, gla_c)
        S.barrier()
        emit_sb(S, ar, psb, dB, ident, sb_c)
        S.barrier()
        S.cc("AllGather", [send2], [gath2], writes=["gath2"], semkey="cc2")
        dC = dict(pjl=pjl, aTl=aTl, gath2=gath2, idx=idx, w_out=w_out[l], w1=w1[l], w2=w2[l],
                  gmlp=prm[:, pb + 16:pb + 32], gsb=prm[:, pb + 32:pb + 36], gn=prm[:, pb + 36:pb + 38], gfin=prm[:, 128:144])
        emit_phaseC(S, xT, ones, ar, psb, dC, final=(l == DEPTH - 1))
        S.barrier()
    xov = xo_d.rearrange("(kc p) t -> p kc t", p=128)
    S.dma("sp", [dict(out=xov[:, kc, :], in_=xT[:, kc, :]) for kc in range(KC)],
          reads=[f"xT{kc}_{h}" for kc in range(KC) for h in range(2)], semkey="xTstore", is_out=True)
    S.emit()
    S.close()
    return nc


def _lay16(v):
    return np.ascontiguousarray(v.reshape(-1, 128).T)


def _t5_bucket(dist):
    max_exact = 16
    dist = np.asarray(dist)
    ratio = np.log(np.maximum(dist, 1).astype(np.float32) / max_exact) / np.float32(np.log(128 / max_exact))
    large = np.minimum(max_exact + (ratio * 16).astype(np.int32), 31)
    return np.where(dist < max_exact, dist, large)


def swa_bias_tiles(rel_bias):
    kk = np.arange(128)[:, None, None]
    jj = np.arange(2)[None, :, None]
    qq = np.arange(128)[None, None, :]
    dist = qq + 128 - (jj * 128 + kk)
    inwin = (dist >= 0) & (dist < 128)
    bucket = _t5_bucket(np.maximum(dist, 0))
    g = rel_bias[bucket]
    g = np.where(inwin[..., None], g, np.float32(-30000.0)).astype(np.float32)
    b = np.ascontiguousarray(g.transpose(0, 3, 1, 2).reshape(128, 12, 256))
    b0 = b.copy()
    b0[:, :, 0:128] = -30000.0
    return b, b0


def index_table(c):
    ix = np.zeros((128, NIDX), np.int32)
    p = np.arange(128)
    row1 = lambda s_, piece, r: (s_ * NX + (piece - X0)) * 128 + r
    p64 = np.minimum(p, 63)
    p96 = np.minimum(p, 95)
    for s_ in range(NCORES):
        ix[:, 0 + s_] = row1(s_, P_CQ + c // 2, 64 * (c % 2) + p64)
        ix[:, 8 + s_] = row1(s_, P_CK + c // 2, 64 * (c % 2) + p64)
        ix[:, 16 + s_] = row1(s_, P_CV + c // 2, 64 * (c % 2) + p64)
        ix[:, 24 + s_] = row1(s_, P_BQ + c // 2, p96)
        ix[:, 32 + s_] = row1(s_, P_BK + c // 2, p96)
        ix[:, 40 + s_] = row1(s_, P_BV + c, p96)
    cm1 = max(c - 1, 0)
    for g in range(4):
        ix[:, 48 + g] = (cm1 * 4 + g // 2) * 128 + 64 * (g % 2) + p64
        ix[:, 52 + g] = (cm1 * 4 + 2 + g // 2) * 128 + 64 * (g % 2) + p64
    row2 = lambda s_, h, r: ((s_ * NCORES + c) * 2 + h) * R2 + r
    for i in range(NCORES):
        for h in range(2):
            ix[:, 56 + 2 * i + h] = row2(i, h, 64 + p96)
    for ci in range(4):
        for h in range(2):
            ix[:, 72 + 2 * ci + h] = np.where(p < 64, row2(2 * ci, h, p64), row2(2 * ci + 1, h, np.maximum(p - 64, 0)))
    return ix


_NC_CACHE = {}


def kernel(**inp):
    inp = {k: np.asarray(v) for k, v in inp.items()}
    cores = list(range(NCORES))
    x = inp["x"][0]
    bias, bias0 = swa_bias_tiles(inp["rel_bias"].astype(np.float32))
    prm = np.zeros((128, 160), np.float32)
    for l in range(DEPTH):
        pb = 64 * l
        prm[:, pb:pb + 16] = _lay16(inp["norm_mix"][l])
        prm[:, pb + 16:pb + 32] = _lay16(inp["norm_mlp"][l])
        prm[:, pb + 32:pb + 36] = inp["sb_out_norm"][l].reshape(4, 128).T
        prm[:96, pb + 36:pb + 38] = inp["gla_norm"][l].reshape(2, 96).T
    prm[:, 128:144] = _lay16(inp["norm_final"])
    gA = np.ascontiguousarray(np.broadcast_to(inp["swa_out_norm"][:, None, :], (DEPTH, 128, 768))).astype(np.float32)
    sinks = np.ascontiguousarray(np.broadcast_to(inp["swa_sinks"][:, None, :], (DEPTH, 128, 12))).astype(np.float32)
    shared = {"w_in": np.ascontiguousarray(inp["w_in"], dtype=np.float32), "w_out": np.ascontiguousarray(inp["w_out"], dtype=np.float32),
              "w1": np.ascontiguousarray(inp["w_mlp_in"], dtype=np.float32), "w2": np.ascontiguousarray(inp["w_mlp_out"], dtype=np.float32),
              "prm": prm, "swb": bias, "gA": gA, "sinks": sinks}
    in_maps = []
    for c in cores:
        hd = c // 2
        gw = np.concatenate([inp["gla_gate_w"][:, :, 96 * hd:96 * hd + 96], inp["gla_gate_b"][:, None, 96 * hd:96 * hd + 96]], axis=1)
        m = dict(shared)
        m.update({"xT": np.ascontiguousarray(x[c * T:(c + 1) * T].T), "gw": np.ascontiguousarray(gw.astype(np.float32)),
                  "swb0": bias0 if c == 0 else bias, "idx": index_table(c)})
        in_maps.append(m)
    if "fused" not in _NC_CACHE:
        _NC_CACHE["fused"] = build_fused()
    res = run_bass_kernel_spmd(_NC_CACHE["fused"], in_maps, core_ids=cores)
    out = np.concatenate([res.results[c]["xoT"].T for c in cores], axis=0)[None]
    return np.ascontiguousarray(out.astype(np.float32))
```
